# Optimizing a Trainium2 kernel written in Bass

```python
import functools
import jax, jax.numpy as jnp
from jax import lax
import numpy as np

D_MODEL = 2048
BATCH = 2
SEQ = 4096
DEPTH = 4
DEC_BATCH = 8
DEC_SEQ = 16
PAST_LEN = 4096

CHUNK = 64
CONV_DIM = D_MODEL // 4
CONV_WIDTH = 31
SWA_HEAD_DIM = 64
SWA_DIM = D_MODEL // 2
SWA_HEADS = SWA_DIM // SWA_HEAD_DIM
SWA_KV_HEADS = 2
SWA_GROUP = SWA_HEADS // SWA_KV_HEADS
SWA_KV_DIM = SWA_KV_HEADS * SWA_HEAD_DIM
WINDOW = 128
WIN_CHUNKS = WINDOW // CHUNK
MEM_TOKENS = 256
MEM_HEADS = 4
MEM_DIM = D_MODEL // 4
MEM_HEAD_DIM = MEM_DIM // MEM_HEADS
MIX_DIM = CONV_DIM + SWA_DIM + MEM_DIM
NUM_BUCKETS = 32
MAX_DISTANCE = 128
EPS = 1e-6
NEG_INF = -1e30
SWA_SCALE = SWA_HEAD_DIM ** -0.5
MEM_SCALE = MEM_HEAD_DIM ** -0.5
IN_SPLITS = (CONV_DIM, CONV_DIM, CONV_DIM, SWA_DIM, SWA_KV_DIM, SWA_KV_DIM, SWA_DIM, MEM_DIM, MEM_DIM)
IN_DIM = sum(IN_SPLITS)

kernel_name = "hybrid_conv_swa_memory_stream_step"


def rms_norm(x, g):
    xf = x.astype(jnp.float32)
    y = xf * lax.rsqrt(jnp.mean(xf * xf, axis=-1, keepdims=True) + EPS)
    return (y * g.astype(jnp.float32)).astype(x.dtype)


def layer_norm(x, g, b):
    xf = x.astype(jnp.float32)
    mu = jnp.mean(xf, axis=-1, keepdims=True)
    xc = xf - mu
    y = xc * lax.rsqrt(jnp.mean(xc * xc, axis=-1, keepdims=True) + EPS)
    return (y * g.astype(jnp.float32) + b.astype(jnp.float32)).astype(x.dtype)


def t5_bucket(rel):
    half = NUM_BUCKETS // 2
    exact = half // 2
    side = jnp.where(rel > 0, half, 0)
    n = jnp.abs(rel)
    nf = jnp.maximum(n, 1).astype(jnp.float32)
    large = exact + (jnp.log(nf / exact) / np.float32(np.log(MAX_DISTANCE / exact)) * (half - exact)).astype(jnp.int32)
    large = jnp.minimum(large, half - 1)
    return side + jnp.where(n < exact, n, large)


def rel_pos_bias(table, n_q, n_k, offset):
    rel = jnp.arange(n_k)[None, :] - offset - jnp.arange(n_q)[:, None]
    b = jnp.take(table, t5_bucket(rel), axis=0)
    b = jnp.transpose(b, (2, 0, 1)).astype(jnp.float32)
    return b.reshape(SWA_KV_HEADS, SWA_GROUP, n_q, n_k)


def sink_attention(q, k, v, bias, mask, sinks):
    s = jnp.einsum('...ikgd,...jkd->...kgij', q, k).astype(jnp.float32) * SWA_SCALE + bias
    if mask is not None:
        s = jnp.where(mask, s, NEG_INF)
    sink = sinks.astype(jnp.float32).reshape(SWA_KV_HEADS, SWA_GROUP, 1, 1)
    m = jnp.maximum(jnp.max(s, axis=-1, keepdims=True), sink)
    p = jnp.exp(s - m)
    w = p / (jnp.sum(p, axis=-1, keepdims=True) + jnp.exp(sink - m))
    return jnp.einsum('...kgij,...jkd->...ikgd', w.astype(v.dtype), v)


def swa_prompt(q, k, v, sinks, bias):
    B, T = q.shape[0], q.shape[1]
    nC = T // CHUNK
    pad = WIN_CHUNKS * CHUNK
    qb = q.reshape(B, nC, CHUNK, SWA_KV_HEADS, SWA_GROUP, SWA_HEAD_DIM)

    def band(t):
        tp = jnp.pad(t, ((0, 0), (pad, 0), (0, 0), (0, 0)))
        tp = tp.reshape(B, nC + WIN_CHUNKS, CHUNK, SWA_KV_HEADS, SWA_HEAD_DIM)
        return jnp.concatenate([tp[:, w:w + nC] for w in range(WIN_CHUNKS + 1)], axis=2)

    kb, vb = band(k), band(v)
    key_pos = jnp.arange(nC)[:, None] * CHUNK + jnp.arange(pad + CHUNK)[None, :] - pad
    mask = (key_pos >= 0)[:, None, None, None, :]
    o = sink_attention(qb, kb, vb, bias, mask, sinks)
    return o.reshape(B, T, SWA_DIM), k[:, -WINDOW:], v[:, -WINDOW:]


def swa_sample(q, k, v, cache_k, cache_v, sinks, bias):
    B, S = q.shape[0], q.shape[1]
    L = cache_k.shape[1]
    qh = q.reshape(B, S, SWA_KV_HEADS, SWA_GROUP, SWA_HEAD_DIM)
    kf = jnp.concatenate([cache_k, k], axis=1)
    vf = jnp.concatenate([cache_v, v], axis=1)
    o = sink_attention(qh, kf, vf, bias, None, sinks)
    return o.reshape(B, S, SWA_DIM), kf[:, -L:], vf[:, -L:]


def memory_kv(mem, mem_g, w_mem_kv):
    B, M = mem.shape[0], mem.shape[1]
    kv = rms_norm(mem, mem_g) @ w_mem_kv
    mk, mv = jnp.split(kv, 2, axis=-1)
    return (mk.reshape(B, M, MEM_HEADS, MEM_HEAD_DIM), mv.reshape(B, M, MEM_HEADS, MEM_HEAD_DIM))


def mem_attention(q, mk, mv):
    B, T = q.shape[0], q.shape[1]
    qh = q.reshape(B, T, MEM_HEADS, MEM_HEAD_DIM)
    s = jnp.einsum('bthd,bmhd->bhtm', qh, mk).astype(jnp.float32) * MEM_SCALE
    p = jax.nn.softmax(s, axis=-1).astype(mv.dtype)
    return jnp.einsum('bhtm,bmhd->bthd', p, mv).reshape(B, T, MEM_DIM)


def causal_dwconv(x_padded, w, b):
    y = lax.conv_general_dilated(x_padded, w[:, None, :], window_strides=(1,), padding='VALID',
                                 dimension_numbers=('NWC', 'WIO', 'NWC'), feature_group_count=CONV_DIM)
    return y + b


def mixer_layer(x, conv_left, swa_fn, mem_k, mem_v, norm_g, w_in, conv_w, conv_b,
                conv_ln_g, conv_ln_b, w_pw, b_pw, w_out):
    B, T = x.shape[0], x.shape[1]
    h = rms_norm(x, norm_g)
    z = h @ w_in
    pts = []
    acc = 0
    for wdt in IN_SPLITS[:-1]:
        acc += wdt
        pts.append(acc)
    a_val, a_glu, a_gate, bq, bk, bv, b_gate, cq, c_gate = jnp.split(z, pts, axis=-1)
    u = a_val * jax.nn.sigmoid(a_glu)
    conv_in = jnp.concatenate([conv_left, u], axis=1)
    new_conv = conv_in[:, -(CONV_WIDTH - 1):]
    c = causal_dwconv(conv_in, conv_w, conv_b)
    c = jax.nn.silu(layer_norm(c, conv_ln_g, conv_ln_b))
    c = c @ w_pw + b_pw
    y_a = c * jax.nn.silu(a_gate)
    k = bk.reshape(B, T, SWA_KV_HEADS, SWA_HEAD_DIM)
    v = bv.reshape(B, T, SWA_KV_HEADS, SWA_HEAD_DIM)
    o_b, new_k, new_v = swa_fn(bq, k, v)
    y_b = o_b * jax.nn.silu(b_gate)
    y_c = mem_attention(cq, mem_k, mem_v) * jax.nn.silu(c_gate)
    y = jnp.concatenate([y_a, y_b, y_c], axis=-1) @ w_out
    return x + y, new_conv, new_k, new_v


def setup_inputs(seed: int = 0) -> dict:
    key = jax.random.key(seed)
    ks = jax.random.split(key, 24)
    f32 = jnp.float32
    n = lambda k, s: jax.random.normal(k, s, f32)
    n_win = min(WINDOW, PAST_LEN)
    return {
        "x_prompt": n(ks[0], (BATCH, SEQ, D_MODEL)),
        "x_sample": n(ks[1], (DEC_BATCH, DEC_SEQ, D_MODEL)),
        "mem_prompt": n(ks[2], (BATCH, MEM_TOKENS, D_MODEL)),
        "cache_swa_k": n(ks[3], (DEPTH, DEC_BATCH, n_win, SWA_KV_HEADS, SWA_HEAD_DIM)),
        "cache_swa_v": n(ks[4], (DEPTH, DEC_BATCH, n_win, SWA_KV_HEADS, SWA_HEAD_DIM)),
        "cache_conv": 0.5 * n(ks[5], (DEPTH, DEC_BATCH, CONV_WIDTH - 1, CONV_DIM)),
        "cache_mem_k": n(ks[6], (DEPTH, DEC_BATCH, MEM_TOKENS, MEM_HEADS, MEM_HEAD_DIM)),
        "cache_mem_v": n(ks[7], (DEPTH, DEC_BATCH, MEM_TOKENS, MEM_HEADS, MEM_HEAD_DIM)),
        "norm_g": 1.0 + 0.02 * n(ks[8], (DEPTH, D_MODEL)),
        "w_in": n(ks[9], (DEPTH, D_MODEL, IN_DIM)) * D_MODEL ** -0.5,
        "conv_w": n(ks[10], (DEPTH, CONV_WIDTH, CONV_DIM)) * CONV_WIDTH ** -0.5,
        "conv_b": 0.02 * n(ks[11], (DEPTH, CONV_DIM)),
        "conv_ln_g": 1.0 + 0.02 * n(ks[12], (DEPTH, CONV_DIM)),
        "conv_ln_b": 0.02 * n(ks[13], (DEPTH, CONV_DIM)),
        "w_pw": n(ks[14], (DEPTH, CONV_DIM, CONV_DIM)) * CONV_DIM ** -0.5,
        "b_pw": 0.02 * n(ks[15], (DEPTH, CONV_DIM)),
        "swa_sinks": n(ks[16], (DEPTH, SWA_HEADS)),
        "mem_norm_g": 1.0 + 0.02 * n(ks[17], (DEPTH, D_MODEL)),
        "w_mem_kv": n(ks[18], (DEPTH, D_MODEL, 2 * MEM_DIM)) * D_MODEL ** -0.5,
        "rel_bias": 0.5 * n(ks[19], (NUM_BUCKETS, SWA_HEADS)),
        "w_out": n(ks[20], (DEPTH, MIX_DIM, D_MODEL)) * MIX_DIM ** -0.5,
        "final_norm_g": 1.0 + 0.02 * n(ks[21], (D_MODEL,)),
    }


def reference(x_prompt, x_sample, mem_prompt, cache_swa_k, cache_swa_v, cache_conv, cache_mem_k,
              cache_mem_v, norm_g, w_in, conv_w, conv_b, conv_ln_g, conv_ln_b, w_pw, b_pw,
              swa_sinks, mem_norm_g, w_mem_kv, rel_bias, w_out, final_norm_g):
    B, T = x_prompt.shape[0], x_prompt.shape[1]
    S = x_sample.shape[1]
    L = cache_swa_k.shape[2]
    band = WIN_CHUNKS * CHUNK
    bias_p = rel_pos_bias(rel_bias, CHUNK, band + CHUNK, band)
    bias_s = rel_pos_bias(rel_bias, S, L + S, L)
    hp, hs = x_prompt, x_sample
    p_k, p_v, p_conv, p_mk, p_mv = [], [], [], [], []
    s_k, s_v, s_conv = [], [], []
    for l in range(DEPTH):
        lw = (norm_g[l], w_in[l], conv_w[l], conv_b[l], conv_ln_g[l], conv_ln_b[l], w_pw[l], b_pw[l], w_out[l])
        mk, mv = memory_kv(mem_prompt, mem_norm_g[l], w_mem_kv[l])
        left = jnp.zeros((B, CONV_WIDTH - 1, CONV_DIM), hp.dtype)
        fn_p = functools.partial(swa_prompt, sinks=swa_sinks[l], bias=bias_p)
        hp, c_new, k_new, v_new = mixer_layer(hp, left, fn_p, mk, mv, *lw)
        p_k.append(k_new)
        p_v.append(v_new)
        p_conv.append(c_new)
        p_mk.append(mk)
        p_mv.append(mv)
        fn_s = functools.partial(swa_sample, cache_k=cache_swa_k[l], cache_v=cache_swa_v[l],
                                 sinks=swa_sinks[l], bias=bias_s)
        hs, c_new, k_new, v_new = mixer_layer(hs, cache_conv[l], fn_s, cache_mem_k[l], cache_mem_v[l], *lw)
        s_k.append(k_new)
        s_v.append(v_new)
        s_conv.append(c_new)
    y_prompt = rms_norm(hp, final_norm_g)
    y_sample = rms_norm(hs, final_norm_g)
    return (y_prompt, y_sample, jnp.stack(p_k), jnp.stack(p_v), jnp.stack(p_conv), jnp.stack(p_mk),
            jnp.stack(p_mv), jnp.stack(s_k), jnp.stack(s_v), jnp.stack(s_conv))
```

```python
import numpy as np
from contextlib import ExitStack
import concourse.bass as bass
import concourse.mybir as mybir
from concourse.bass_utils import run_bass_kernel_spmd

F32 = mybir.dt.float32
BF16 = mybir.dt.bfloat16
ALU = mybir.AluOpType
AF = mybir.ActivationFunctionType

L = 4
D = 2048
KC = 16
TW = 1536
NCH = 38
NPAR = 172
EPS = 1e-6
NEG = -30000.0
NSLOT = 6


class Tracker:
    def __init__(self):
        self.ops = []
        self.last_w = {}
        self.readers = {}
        self.last_dma_on_key = {}

    def op(self, eng, fn, r=(), w=(), dkey=None):
        xr = tuple(b for b in r if isinstance(b, tuple) and b[0] == "ps")
        r = tuple(b for b in r if not (isinstance(b, tuple) and b[0] == "ps"))
        idx = len(self.ops)
        deps = set()
        for b in r:
            if b in self.last_w:
                deps.add(self.last_w[b])
        for b in xr:
            if b in self.last_w:
                deps.add(self.last_w[b])
            for y in self.readers.get(b, ()):
                deps.add(y)
        for b in w:
            if b in self.last_w:
                deps.add(self.last_w[b])
            for x in self.readers.get(b, ()):
                deps.add(x)
        if dkey is not None and dkey in self.last_dma_on_key:
            deps.add(self.last_dma_on_key[dkey])
        for b in r:
            self.readers.setdefault(b, []).append(idx)
        for b in w:
            self.last_w[b] = idx
            self.readers[b] = []
        for b in xr:
            self.last_w[b] = idx
            self.readers[b] = []
        if dkey is not None:
            self.last_dma_on_key[dkey] = idx
        rset = set(r) | set(xr)
        self.ops.append(dict(eng=eng, fn=fn, deps=deps, dkey=dkey, rset=rset, wset=set(w)))
        return idx

    def finalize(self):
        ops = self.ops
        for i, o in enumerate(ops):
            keep = set()
            for d in o["deps"]:
                od = ops[d]
                if od["dkey"] is None and o["dkey"] is None and od["eng"] == o["eng"]:
                    if od.get("wset") and (od["wset"] & o["rset"]):
                        keep.add(d)
                    continue
                if od["dkey"] is None and o["dkey"] is not None and od["eng"] == o["eng"]:
                    keep.add(d)
                    continue
                keep.add(d)
            best = {}
            for d in keep:
                od = ops[d]
                k = ("d", od["dkey"]) if od["dkey"] is not None else ("e", od["eng"])
                if k not in best or best[k] < d:
                    best[k] = d
            o["deps"] = set(best.values())
        milestone = [False] * len(ops)
        for o in ops:
            for d in o["deps"]:
                milestone[d] = True
        cnt = {}
        dcnt = {}
        for i, o in enumerate(ops):
            if o["dkey"] is not None:
                dcnt[o["dkey"]] = dcnt.get(o["dkey"], 0) + 16
                o["tok"] = (("d", o["dkey"]), dcnt[o["dkey"]])
            else:
                if milestone[i]:
                    cnt[o["eng"]] = cnt.get(o["eng"], 0) + 1
                    o["tok"] = (("e", o["eng"]), cnt[o["eng"]])
                    o["inc"] = True
                else:
                    o["tok"] = None
                    o["inc"] = False
        self.dcnt = dcnt
        return dcnt


def prepare(x_prompt, x_sample, mem_prompt, cache_swa_k, cache_swa_v, cache_conv, cache_mem_k,
           cache_mem_v, norm_g, w_in, conv_w, conv_b, conv_ln_g, conv_ln_b, w_pw, b_pw,
           swa_sinks, mem_norm_g, w_mem_kv, rel_bias, w_out, final_norm_g):
    f = lambda a: np.ascontiguousarray(np.asarray(a, dtype=np.float32))
    x_prompt, x_sample, mem_prompt = f(x_prompt), f(x_sample), f(mem_prompt)
    cache_swa_k, cache_swa_v, cache_conv = f(cache_swa_k), f(cache_swa_v), f(cache_conv)
    cache_mem_k, cache_mem_v = f(cache_mem_k), f(cache_mem_v)
    w_in, w_out, w_pw, w_mem_kv = f(w_in), f(w_out), f(w_pw), f(w_mem_kv)

    o_aval, o_aglu, o_agate, o_bq, o_bk, o_bv, o_bgate, o_cq, o_cgate = 0, 512, 1024, 1536, 2560, 2688, 2816, 3840, 4352
    cols = []
    cols += list(range(o_aval, o_aval + 512))
    cols += list(range(o_aglu, o_aglu + 512))
    cols += list(range(o_agate, o_agate + 512))
    cols += list(range(o_bk, o_bk + 128))
    cols += list(range(o_bv, o_bv + 128))
    for base in (o_bq,):
        for g in range(8):
            cols += list(range(base + g * 64, base + g * 64 + 64))
            cols += list(range(base + (8 + g) * 64, base + (8 + g) * 64 + 64))
    for base in (o_bgate,):
        for g in range(8):
            cols += list(range(base + g * 64, base + g * 64 + 64))
            cols += list(range(base + (8 + g) * 64, base + (8 + g) * 64 + 64))
    cols += list(range(o_cq, o_cq + 512))
    cols += list(range(o_cgate, o_cgate + 512))
    cols = np.array(cols)
    assert cols.shape[0] == 4864
    rowmap = []
    rowmap += list(range(0, 512))
    for g in range(8):
        rowmap += list(range(512 + g * 64, 512 + g * 64 + 64))
        rowmap += list(range(512 + (8 + g) * 64, 512 + (8 + g) * 64 + 64))
    rowmap += list(range(1536, 2048))
    rowmap = np.array(rowmap)

    w_in_r = np.ascontiguousarray(
        w_in[:, :, cols].reshape(L, KC, 128, NCH, 128).transpose(0, 3, 2, 1, 4)).reshape(L, NCH, 128, KC * 128)
    w_out_r = np.ascontiguousarray(
        w_out[:, rowmap, :].reshape(L, KC, 128, KC, 128).transpose(0, 3, 2, 1, 4)).reshape(L, KC, 128, KC * 128)
    w_pw_r = np.ascontiguousarray(
        w_pw.reshape(L, 4, 128, 4, 128).transpose(0, 2, 3, 1, 4)).reshape(L, 128, 4 * 512)
    w_mkv_r = np.ascontiguousarray(w_mem_kv.reshape(L, KC, 128, 1024))

    def pc(v, n):
        return np.asarray(v, np.float32).reshape(n, 128).T

    params = np.zeros((L, 128, NPAR), np.float32)
    for l in range(L):
        params[l, :, 0:16] = pc(norm_g[l], 16)
        params[l, :, 16:32] = pc(mem_norm_g[l], 16)
        params[l, :, 32:36] = pc(conv_b[l], 4)
        params[l, :, 36:40] = pc(conv_ln_g[l], 4)
        params[l, :, 40:44] = pc(conv_ln_b[l], 4)
        params[l, :, 44:48] = pc(b_pw[l], 4)
        cw = np.asarray(conv_w[l], np.float32)
        params[l, :, 48:172] = cw.T.reshape(4, 128, 31).transpose(1, 0, 2).reshape(128, 124)
    fparams = np.ascontiguousarray(pc(final_norm_g, 16))
    sinks_b = np.ascontiguousarray(np.broadcast_to(np.asarray(swa_sinks, np.float32)[:, None, :], (L, 128, 16)))
    table = f(rel_bias)

    def t5_np(rel):
        half, exact = 16, 8
        side = np.where(rel > 0, half, 0)
        n = np.abs(rel)
        nf = np.maximum(n, 1).astype(np.float32)
        large = exact + (np.log(nf / np.float32(exact)) / np.float32(np.log(128 / 8)) * np.float32(half - exact)).astype(np.int32)
        large = np.minimum(large, half - 1)
        return side + np.where(n < exact, n, large)

    def onehot(rels):
        b = t5_np(rels)
        oh = np.zeros((32, rels.shape[0]), np.float32)
        oh[b, np.arange(rels.shape[0])] = 1.0
        return oh

    oh_p = onehot(np.arange(255) - 63 - 128)
    oh_s = onehot(np.arange(159) - 15 - 128)
    ident = np.eye(128, dtype=np.float32)

    in_maps = []
    for c in range(8):
        b, q = c // 4, c % 4
        p0 = q * 1024 - 512
        xw = np.zeros((TW, D), np.float32)
        lo = max(p0, 0)
        xw[lo - p0:] = x_prompt[b, lo:p0 + TW]
        xT = np.ascontiguousarray(xw.T.reshape(KC, 128, TW).transpose(1, 0, 2))
        maskw = np.zeros((128, 24), np.float32)
        if q == 0:
            for w_ in range(24):
                for j in range(128):
                    if w_ * 64 + j < 512:
                        maskw[j, w_] = NEG
        memT = np.ascontiguousarray(mem_prompt[b].T.reshape(KC, 128, 256).transpose(1, 0, 2))
        xsT = np.ascontiguousarray(x_sample[c].T.reshape(KC, 128, 16).transpose(1, 0, 2))
        ck = cache_swa_k[:, c].reshape(L, 128, 128)
        cv = cache_swa_v[:, c].reshape(L, 128, 128)
        ckT = np.ascontiguousarray(ck.transpose(0, 2, 1))
        cvT = np.ascontiguousarray(cv.transpose(0, 2, 1))
        ccT = np.ascontiguousarray(cache_conv[:, c].transpose(0, 2, 1).reshape(L, 4, 128, 30).transpose(0, 2, 1, 3))
        cmkT = np.ascontiguousarray(cache_mem_k[:, c].transpose(0, 3, 2, 1))
        cmv = np.ascontiguousarray(cache_mem_v[:, c].reshape(L, 2, 128, 512).transpose(0, 2, 1, 3))
        in_maps.append(dict(
            xT=xT, maskw=maskw, memT=memT, xsT=xsT, ckT=ckT, cvT=cvT, cv=np.ascontiguousarray(cv), ccT=ccT,
            cmkT=cmkT, cmv=cmv, w_in_r=w_in_r, w_out_r=w_out_r, w_pw_r=w_pw_r, w_mkv_r=w_mkv_r,
            params=params, fparams=fparams, sinks_b=sinks_b, table=table, oh_p=oh_p, oh_s=oh_s, ident=ident))

    return in_maps


def kernel(**inputs):
    in_maps = prepare(**inputs)
    nc = build_program()
    res = run_bass_kernel_spmd(nc, in_maps, core_ids=list(range(8)))
    return assemble(res.results)


def assemble(R, cores=range(8)):

    y_prompt = np.zeros((2, 4096, D), np.float32)
    y_sample = np.zeros((8, 16, D), np.float32)
    nk_p = np.zeros((L, 2, 128, 2, 64), np.float32)
    nv_p = np.zeros((L, 2, 128, 2, 64), np.float32)
    nc_p = np.zeros((L, 2, 30, 512), np.float32)
    nmk_p = np.zeros((L, 2, 256, 4, 128), np.float32)
    nmv_p = np.zeros((L, 2, 256, 4, 128), np.float32)
    nk_s = np.zeros((L, 8, 128, 2, 64), np.float32)
    nv_s = np.zeros((L, 8, 128, 2, 64), np.float32)
    nc_s = np.zeros((L, 8, 30, 512), np.float32)
    for c in cores:
        b, q = c // 4, c % 4
        r = R[c]
        y_prompt[b, q * 1024:(q + 1) * 1024] = np.asarray(r["yT"]).reshape(128, KC, 1024).transpose(2, 1, 0).reshape(1024, D)
        y_sample[c] = np.asarray(r["ysT"]).reshape(128, KC, 16).transpose(2, 1, 0).reshape(16, D)
        oks = np.asarray(r["oks"]).reshape(L, 128, 128)
        ovs = np.asarray(r["ovs"]).reshape(L, 128, 128)
        ocs = np.asarray(r["ocs"]).reshape(L, 128, 4, 30)
        nk_s[:, c] = oks.transpose(0, 2, 1).reshape(L, 128, 2, 64)
        nv_s[:, c] = ovs.transpose(0, 2, 1).reshape(L, 128, 2, 64)
        nc_s[:, c] = ocs.transpose(0, 3, 2, 1).reshape(L, 30, 512)
        if q == 3:
            okp = np.asarray(r["okp"]).reshape(L, 128, 128)
            ovp = np.asarray(r["ovp"]).reshape(L, 128, 128)
            ocp = np.asarray(r["ocp"]).reshape(L, 128, 4, 30)
            nk_p[:, b] = okp.transpose(0, 2, 1).reshape(L, 128, 2, 64)
            nv_p[:, b] = ovp.transpose(0, 2, 1).reshape(L, 128, 2, 64)
            nc_p[:, b] = ocp.transpose(0, 3, 2, 1).reshape(L, 30, 512)
        if q == 0:
            omkv = np.asarray(r["omkv"]).reshape(L, 128, 2, 1024)
            kvt = omkv.transpose(0, 2, 1, 3).reshape(L, 256, 1024)
            nmk_p[:, b] = kvt[:, :, 0:512].reshape(L, 256, 4, 128)
            nmv_p[:, b] = kvt[:, :, 512:1024].reshape(L, 256, 4, 128)
    return (y_prompt, y_sample, nk_p, nv_p, nc_p, nmk_p, nmv_p, nk_s, nv_s, nc_s)


STOP = None
DBG = set()


class _Stop(Exception):
    pass


def ck(k):
    if STOP is not None and STOP == k:
        raise _Stop()


def build_program():
    nc = bass.Bass("TRN2", target_bir_lowering=False)
    T = Tracker()
    es = ExitStack()
    try:
        _build_body(nc, T, es)
    except _Stop:
        pass
    emit(nc, T, es)
    return nc


def _build_body(nc, T, es):

    def din(name, shape):
        return nc.dram_tensor(name, list(shape), F32, kind="ExternalInput").ap()

    def dout(name, shape):
        return nc.dram_tensor(name, list(shape), F32, kind="ExternalOutput").ap()

    xT_d = din("xT", [128, KC, TW]); maskw_d = din("maskw", [128, 24]); memT_d = din("memT", [128, KC, 256])
    xsT_d = din("xsT", [128, KC, 16]); ckT_d = din("ckT", [L, 128, 128]); cvT_d = din("cvT", [L, 128, 128])
    cv_d = din("cv", [L, 128, 128]); ccT_d = din("ccT", [L, 128, 4, 30]); cmkT_d = din("cmkT", [L, 128, 4, 256])
    cmv_d = din("cmv", [L, 128, 2, 512]); w_in_d = din("w_in_r", [L, NCH, 128, KC * 128])
    w_out_d = din("w_out_r", [L, KC, 128, KC * 128]); w_pw_d = din("w_pw_r", [L, 128, 2048])
    w_mkv_d = din("w_mkv_r", [L, KC, 128, 1024]); params_d = din("params", [L, 128, NPAR])
    fparams_d = din("fparams", [128, 16]); sinks_d = din("sinks_b", [L, 128, 16]); table_d = din("table", [32, 16])
    ohp_d = din("oh_p", [32, 255]); ohs_d = din("oh_s", [32, 159]); ident_d = din("ident", [128, 128])
    yT_o = dout("yT", [128, KC, 1024]); ysT_o = dout("ysT", [128, KC, 16])
    okp_o = dout("okp", [L, 128, 128]); ovp_o = dout("ovp", [L, 128, 128]); ocp_o = dout("ocp", [L, 128, 4, 30])
    omkv_o = dout("omkv", [L, 128, 2, 1024]); oks_o = dout("oks", [L, 128, 128]); ovs_o = dout("ovs", [L, 128, 128])
    ocs_o = dout("ocs", [L, 128, 4, 30])
    xscr = nc.dram_tensor("xscr", [128, KC, TW], F32).ap()

    def sb(name, shape, dt):
        return es.enter_context(nc.sbuf_tensor(name, list(shape), dt))

    hT = sb("hT", [128, KC, TW], BF16)
    gy = sb("gy", [128, KC, 1408], BF16)
    ZB = 54 * 1024
    zt = sb("zt", [128, ZB], mybir.dt.uint8)
    ring = [sb("ring%d" % i, [128, KC * 128], BF16) for i in range(NSLOT)]
    identb = sb("identb", [128, 128], BF16)
    ones = sb("ones", [128, 128], BF16)
    onesdiv = sb("onesdiv", [128, 128], BF16)
    onesD = sb("onesD", [128, 128], BF16)
    onesD32 = sb("onesD32", [128, 128], F32)
    Ep = sb("Ep", [128, 2, 2, 512], BF16)
    Es = sb("Es", [128, 2, 2, 128], BF16)
    maskw = sb("maskw_s", [128, 24], F32)
    par = [sb("par%d" % i, [128, NPAR], F32) for i in range(2)]
    fpar = sb("fpar", [128, 16], F32)
    snk = sb("snk", [128, 16], F32)
    esk = sb("esk", [128, 16], F32)
    xs_res = sb("xs_res", [128, KC, 16], F32)
    hsT = sb("hsT", [128, KC, 16], BF16)
    gys = sb("gys", [128, KC, 16], BF16)
    usT = sb("usT", [128, 4, 46], BF16)
    us32 = sb("us32", [128, 4, 30], F32)
    qsT = sb("qsT", [128, 8, 16], BF16)
    ksT = sb("ksT", [128, 144], BF16)
    vsT = sb("vsT", [128, 16], BF16)
    vstok = sb("vstok", [16, 256], BF16)
    cqs = sb("cqs", [128, 4, 16], BF16)
    ok32 = sb("ok32", [128, 128], F32)
    ov32 = sb("ov32", [128, 128], F32)
    ck32 = sb("ck32", [128, 128], F32)
    cv32T = sb("cv32T", [128, 128], F32)
    cvb = sb("cvb", [128, 256], BF16)
    cc32 = sb("cc32", [128, 4, 30], F32)
    cmkb = sb("cmkb", [128, 4, 256], BF16)
    cmvb = sb("cmvb", [128, 2, 512], BF16)
    kvb = sb("kvb", [128, 2, 1024], BF16)
    mkT = sb("mkT", [128, 4, 256], BF16)
    wpw = sb("wpw", [128, 4, 512], BF16)
    k32 = sb("k32", [128, 128], F32)
    v32 = sb("v32", [128, 128], F32)
    u32 = sb("u32", [128, 4, 30], F32)
    sinkrow = sb("sinkrow", [1, 2, 512], BF16)
    sinkrow_s = sb("sinkrow_s", [1, 2, 128], BF16)
    selr = sb("selr", [1, 2, 128], BF16)
    psb = [es.enter_context(nc.psum_tensor("ps%d" % i, [128, 512], F32)) for i in range(8)]

    class Carver:
        def __init__(self):
            self.off = 0

        def get(self, shape, dt):
            n = int(np.prod(shape[1:]))
            bpe = 2 if dt == BF16 else 4
            nb = n * bpe
            self.off = (self.off + 63) // 64 * 64
            assert self.off + nb <= ZB, ("zt overflow", self.off + nb)
            ap = zt[:, self.off:self.off + nb].bitcast(dt)
            self.off += nb
            if len(shape) == 3:
                ap = ap.rearrange("p (a b) -> p a b", a=shape[1])
            return ap

    ZA = ("ztall",)
    bank_rr = [0]

    def nbank():
        b = bank_rr[0] % 8
        bank_rr[0] += 1
        return b

    slot_rr = [0]

    def nslot():
        s = slot_rr[0] % NSLOT
        slot_rr[0] += 1
        return s

    def I(eng, meth, r, w, *args, **kw):
        return T.op(eng, (lambda e: getattr(e, meth)(*args, **kw)), r=tuple(r), w=tuple(w))

    def dma(q, out, in_, key, r=(), w=()):
        return T.op(q, (lambda e: e.dma_start(out=out, in_=in_)), r=tuple(r), w=tuple(w), dkey=key)

    barscr = sb("barscr", [1, 16], F32)

    def phase_barrier():
        T.op("act", (lambda e: e.activation(out=barscr[0:1, 0:8], in_=barscr[0:1, 8:16], func=AF.Copy)), r=(), w=ZA)

    def P(b):
        return ("ps", b)

    cz0 = Carver()
    ohp = cz0.get([128, 255], F32)
    ohs = cz0.get([128, 159], F32)
    tbl = cz0.get([128, 16], F32)
    dma("pool", identb[:], ident_d, "identb", w=("identb",))
    dma("sp", ohp[0:32, :], ohp_d, "ohp", r=ZA, w=("ohp",))
    dma("sp", ohs[0:32, :], ohs_d, "ohs", r=ZA, w=("ohs",))
    dma("sp", tbl[0:32, :], table_d, "tbl", r=ZA, w=("tbl",))
    dma("sp", maskw[:], maskw_d, "maskw", w=("maskw",))
    dma("sp", fpar[:], fparams_d, "fpar", w=("fpar",))
    dma("sp", xs_res[:], xsT_d, "xs_res", w=("xs_res",))
    I("dve", "memset", (), ("ones",), ones[:], 1.0)
    I("dve", "memset", (), ZA, barscr[:], 0.0)
    I("dve", "memset", (), ("onesdiv",), onesdiv[:], 1.0 / 512)
    I("dve", "memset", (), ("onesD",), onesD[:], 1.0 / 2048)
    I("dve", "memset", (), ("onesD32",), onesD32[:], 1.0 / 2048)
    I("dve", "memset", (), ("selr",), selr[0:1, 0, 0:64], 0.0)
    I("dve", "memset", (), ("selr",), selr[0:1, 0, 64:128], 1.0)
    I("dve", "memset", (), ("selr",), selr[0:1, 1, 0:64], 1.0)
    I("dve", "memset", (), ("selr",), selr[0:1, 1, 64:128], 0.0)
    I("dve", "memset", (), ("cvb",), cvb[:, 64:192], 1.0)
    I("dve", "memset", (), ("vstok",), vstok[:, 64:192], 1.0)

    def build_E(dst, oh, ohname, ni, roff, blocks):
        for kv in range(2):
            for bi, (j0, nj) in enumerate(blocks):
                b = nbank()
                for i in range(ni):
                    outap = psb[b][0:nj, 0:8 * ni].rearrange("p (g i) -> p g i", g=8)[:, :, i:i + 1]
                    lhs = oh[0:32, roff - i + j0: roff - i + j0 + nj]
                    rhs = tbl[0:32, kv * 8:kv * 8 + 8].unsqueeze(2)
                    I("pe", "matmul", (ohname, "tbl") + ZA, (P(b),), outap, lhsT=lhs, rhs=rhs, start=True, stop=True)
                I("act", "activation", (P(b),), ("E",), out=dst[0:nj, kv, bi, 0:8 * ni], in_=psb[b][0:nj, 0:8 * ni], func=AF.Exp)

    build_E(Ep, ohp, "ohp", 64, 63, [(0, 128), (128, 64)])
    build_E(Es, ohs, "ohs", 16, 15, [(0, 128), (128, 16)])
    ck("init")

    def rms_norm_tile(src, n, gcols, gid, dst, src_ids, dst_ids, czn, part=None):
        sq = czn["sq"][:, :, 0:n]
        rs = czn["rs"][:, 0:n]
        if part in (None, "sq"):
            I("act", "activation", tuple(src_ids) + ZA, ("n_sq",), out=sq, in_=src, func=AF.Square)
        if part == "sq":
            return
        b = nbank()
        for c in range(KC):
            I("pe", "matmul", ("n_sq", "onesD") + ZA, (P(b),), psb[b][:, 0:n], lhsT=onesD[:], rhs=sq[:, c, :],
              start=(c == 0), stop=(c == KC - 1))
        I("act", "activation", (P(b),) + ZA, ("n_rs",), out=rs, in_=psb[b][:, 0:n], func=AF.Ln, bias=EPS, scale=1.0)
        I("act", "activation", ("n_rs",) + ZA, ("n_rs",), out=rs, in_=rs, func=AF.Exp, scale=-0.5)
        NDV = 16
        for c in range(KC):
            eng = "dve" if c < NDV else "pool"
            ids = tuple((d, "lo" if c < NDV else "hi") for d in dst_ids)
            I(eng, "scalar_tensor_tensor", tuple(src_ids) + (gid, "n_rs") + ZA, ids, out=dst[:, c, :], in0=src[:, c, :],
              scalar=gcols[:, c:c + 1], in1=rs, op0=ALU.mult, op1=ALU.mult)

    def both(ids):
        return tuple((d, "lo") for d in ids) + tuple((d, "hi") for d in ids)

    def load_w(src_ap, ncols=KC * 128):
        s = nslot()
        dma("pool", ring[s][:, 0:ncols], src_ap, ("ring", s), w=(("ring", s),))
        return s

    def hids(t0, n):
        return both(tuple(("h", j) for j in range(t0 // 128, (t0 + n + 127) // 128)))

    for l in range(L):
        kv0 = 128 * l
        m0 = 128 * (l + 1)
        P_ = par[l % 2]
        tiles_main = ([(m0, 512 - m0)] if m0 < 512 else []) + [(512, 512), (1024, 512)]
        tiles_kv = [(kv0, 512 - kv0), (512, 512), (1024, 512)]
        xsrc = xT_d if l == 0 else xscr

        if l == 0:
            dma("sp", P_[:], params_d[l], ("par", l % 2), w=("par",))
        dma("sp", snk[:], sinks_d[l], "snk", w=("snk",))
        I("act", "activation", ("snk",), ("esk",), out=esk[:], in_=snk[:], func=AF.Exp)
        dma("sp", ck32[:], ckT_d[l], "ck32", w=("ck32",))
        dma("sp", cv32T[:], cvT_d[l], "cv32T", w=("cv32T",))
        dma("sp", cc32[:], ccT_d[l], "cc32", w=("cc32",))
        dma("pool", cvb[:, 0:64], cv_d[l][:, 0:64], "cvb", w=("cvb",))
        dma("pool", cvb[:, 192:256], cv_d[l][:, 64:128], "cvb", w=("cvb",))
        dma("pool", cmkb[:], cmkT_d[l], "cmkb", w=("cmkb",))
        dma("pool", cmvb[:], cmv_d[l], "cmvb", w=("cmvb",))
        dma("pool", wpw[:], w_pw_d[l].rearrange("p (m f) -> p m f", m=4), "wpw", w=("wpw",))

        phase_barrier()
        cz = Carver()
        rstdn = cz.get([128, 1408], F32)
        memnT = cz.get([128, KC, 256], BF16)
        NXN = 3
        xn = [cz.get([128, KC, 128], F32) for _ in range(NXN)]
        czn = dict(sq=cz.get([128, KC, 128], BF16), rs=cz.get([128, 128], F32))
        kv32 = cz.get([128, 2, 1024], F32)
        if l >= 1:
            hw_all = tuple((("h", j), "lo") for j in range(kv0 // 128, 12))
            for kc in range(KC):
                I("dve", "tensor_tensor", hw_all + ("rstdn",) + ZA, (("hc", kc),), out=hT[:, kc, kv0:TW], in0=hT[:, kc, kv0:TW],
                  in1=rstdn[:, kv0 - 128:TW - 128], op=ALU.mult)
        rms_norm_tile(xs_res[:], 16, P_[:, 0:16], "par", hsT[:], ("xs_res",), ("hs",), czn)
        if l == 0:
            for j in range(2):
                bi = j % NXN
                dma("sp", xn[bi], memT_d[:, :, j * 128:(j + 1) * 128], ("xn", bi), r=ZA, w=(("xn", bi),))
                rms_norm_tile(xn[bi], 128, P_[:, 16:32], "par", memnT[:, :, j * 128:(j + 1) * 128], (("xn", bi),), ("memn",), czn)
        if l == 0:
            for j in range(kv0 // 128, 12):
                bi = (j + 2) % NXN
                dma("sp", xn[bi], xsrc[:, :, j * 128:(j + 1) * 128], ("xn", bi),
                    r=ZA + tuple(("xd", c, j // 4) for c in range(KC)), w=(("xn", bi),))
                rms_norm_tile(xn[bi], 128, P_[:, 0:16], "par", hT[:, :, j * 128:(j + 1) * 128], (("xn", bi),), (("h", j),), czn)

        ck("N%d" % l)
        mb = [nbank() for _ in range(4)]
        for kc in range(KC):
            if kc % 2 == 0:
                s = nslot()
                dma("pool", ring[s][:, :].rearrange("p (k f) -> p k f", k=2), w_mkv_d[l, kc:kc + 2].rearrange("k p f -> p k f"),
                    ("ring", s), w=(("ring", s),))
            for mt in range(2):
                if "M_nomm" in DBG:
                    continue
                for cb in range(2):
                    b = mb[mt * 2 + cb]
                    I("pe", "matmul", both(("memn",)) + (("ring", s),) + ZA, (P(b),), psb[b][:, :], lhsT=memnT[:, kc, mt * 128:(mt + 1) * 128],
                      rhs=ring[s][:, (kc % 2) * 1024 + cb * 512:(kc % 2) * 1024 + (cb + 1) * 512], start=(kc == 0), stop=(kc == KC - 1))
        for mt in range(2):
            if "M_noevac" in DBG or "M_nomm" in DBG:
                continue
            for cb in range(2):
                b = mb[mt * 2 + cb]
                I("act", "activation", (P(b),) + ZA, ("kv32",), out=kv32[:, mt, cb * 512:(cb + 1) * 512], in_=psb[b][:, :], func=AF.Copy)
                I("dve", "tensor_copy", (P(b),), ("kvb",), out=kvb[:, mt, cb * 512:(cb + 1) * 512], in_=psb[b][:, :])
        ck("Ma%d" % l)
        dma("sp", omkv_o[l], kv32, "kv32", r=("kv32",) + ZA)
        ck("Mb%d" % l)
        for mt in range(2):
            b = nbank()
            pT = psb[b][:, 0:256].bitcast(BF16)
            for h in range(4):
                I("pe", "transpose", ("kvb", "identb"), (P(b),), pT[:, h * 128:(h + 1) * 128], kvb[:, mt, h * 128:(h + 1) * 128], identb[:])
            I("dve", "tensor_copy", (P(b),), ("mkT",), out=mkT[:, :, mt * 128:(mt + 1) * 128], in_=pT.rearrange("p (h m) -> p h m", h=4))

        ck("M%d" % l)
        def in_proj(m, kvtype, evac, evac_s):
            s = load_w(w_in_d[l, m])
            tiles = tiles_kv if kvtype else tiles_main
            banks = [nbank() for _ in tiles]
            sbk = nbank()
            for kc in range(KC):
                lhs = ring[s][:, kc * 128:(kc + 1) * 128]
                for (t0, n), b in zip(tiles, banks):
                    I("pe", "matmul", (("ring", s), ("hc", kc)) + hids(t0, n), (P(b),), psb[b][:, 0:n], lhsT=lhs, rhs=hT[:, kc, t0:t0 + n],
                      start=(kc == 0), stop=(kc == KC - 1))
                I("pe", "matmul", (("ring", s),) + both(("hs",)), (P(sbk),), psb[sbk][:, 0:16], lhsT=lhs, rhs=hsT[:, kc, :],
                  start=(kc == 0), stop=(kc == KC - 1))
            for (t0, n), b in zip(tiles, banks):
                evac(t0, n, b)
            evac_s(sbk)

        phase_barrier()
        cz = Carver()
        uT = cz.get([128, 4, TW], BF16)
        sg32 = cz.get([128, TW + 16], F32)
        dgs = [cz.get([128, 31, 128], BF16) for _ in range(2)]
        dg_rr = [0]
        cvos = [cz.get([128, 4, 528], F32) for _ in range(2)]
        LT = 256
        cbb = cz.get([128, 4, LT], BF16)
        sg_off = cz.off
        cza = Carver()
        cza.off = 4 * TW * 2
        sqb = cza.get([128, 4, LT], BF16)
        m32 = cza.get([128, LT], F32)
        var = cza.get([128, LT], F32)
        pw32 = cza.get([128, LT], F32)
        assert cza.off <= 4 * TW * 2 + (TW + 16) * 4

        for c in range(4):
            def ev_glu(t0, n, b):
                I("act", "activation", (P(b),) + ZA, ("sg",), out=sg32[:, t0:t0 + n], in_=psb[b][:, 0:n], func=AF.Sigmoid)

            def ev_glu_s(b):
                I("act", "activation", (P(b),) + ZA, ("sg",), out=sg32[:, TW:TW + 16], in_=psb[b][:, 0:16], func=AF.Sigmoid)

            def ev_val(t0, n, b, c=c):
                I("dve", "tensor_tensor", (P(b), "sg") + ZA, (("u", c),), out=uT[:, c, t0:t0 + n], in0=psb[b][:, 0:n],
                  in1=sg32[:, t0:t0 + n], op=ALU.mult)
                if t0 == 1024:
                    I("dve", "tensor_tensor", (P(b), "sg") + ZA, ("u32",), out=u32[:, c, :], in0=psb[b][:, 482:512],
                      in1=sg32[:, 1506:1536], op=ALU.mult)

            def ev_val_s(b, c=c):
                I("dve", "tensor_tensor", (P(b), "sg") + ZA, ("us32",), out=us32[:, c, 14:30], in0=psb[b][:, 0:16],
                  in1=sg32[:, TW:TW + 16], op=ALU.mult)

            def ev_gate(t0, n, b, c=c):
                I("act", "activation", (P(b),), (("gy", c),), out=gy[:, c, t0 - 128:t0 - 128 + n], in_=psb[b][:, 0:n], func=AF.Silu)

            def ev_gate_s(b, c=c):
                I("act", "activation", (P(b),), (("gys", c),), out=gys[:, c, :], in_=psb[b][:, 0:16], func=AF.Silu)

            in_proj(4 + c, True, ev_glu, ev_glu_s)
            in_proj(0 + c, True, ev_val, ev_val_s)
            in_proj(8 + c, False, ev_gate, ev_gate_s)
        dma("sp", ocp_o[l], u32[:], "u32", r=("u32",))
        I("dve", "tensor_copy", ("cc32",), ("us32",), out=us32[:, :, 0:14], in_=cc32[:, :, 16:30])
        I("dve", "tensor_copy", ("cc32",), ("usT",), out=usT[:, :, 0:30], in_=cc32[:, :, :])
        I("dve", "tensor_copy", ("us32",), ("usT",), out=usT[:, :, 30:46], in_=us32[:, :, 14:30])
        dma("sp", ocs_o[l], us32[:], "us32", r=("us32",))

        ck("A1_%d" % l)
        phase_barrier()
        groups = [[t] for t in tiles_main[:-1]] + [tiles_main[-1:] + [("s", 16)]]

        def gcols_of(grp):
            col = 0
            res = []
            for (t0, n) in grp:
                res.append(col)
                col += n
            return res

        def conv_stage(gi):
            grp = groups[gi]
            cvo = cvos[gi % 2]
            cid = ("cvo", gi % 2)
            gcols = gcols_of(grp)
            for c in range(4):
                dgi = dg_rr[0] % 2
                dg_rr[0] += 1
                dg = dgs[dgi]
                I("dve", "tensor_tensor", ("identb", "par") + ZA, (("dg", dgi),), out=dg,
                  in0=identb[:].unsqueeze(1).to_broadcast([128, 31, 128]),
                  in1=P_[:, 48 + c * 31:48 + (c + 1) * 31].unsqueeze(2).to_broadcast([128, 31, 128]), op=ALU.mult)
                for (t0, n), gc in zip(grp, gcols):
                    b = nbank()
                    for k in range(31):
                        if t0 == "s":
                            rhs = usT[:, c, k:k + 16]
                            rid = ("usT",)
                        else:
                            rhs = uT[:, c, t0 - 30 + k:t0 - 30 + k + n]
                            rid = (("u", c),)
                        I("pe", "matmul", (("dg", dgi),) + rid + ZA, (P(b),), psb[b][:, 0:n], lhsT=dg[:, k, :], rhs=rhs,
                          start=(k == 0), stop=(k == 30))
                    I("act", "activation", (P(b), "par") + ZA, (cid,), out=cvo[:, c, gc:gc + n], in_=psb[b][:, 0:n],
                      func=AF.Identity, bias=P_[:, 32 + c:33 + c], scale=1.0)

        def ln_stage(gi):
            grp = groups[gi]
            cvo = cvos[gi % 2]
            cid = ("cvo", gi % 2)
            gcols = gcols_of(grp)
            for (t0, n), gc in zip(grp, gcols):
                for s0 in range(0, n, LT):
                    sn = min(LT, n - s0)
                    cs = gc + s0
                    I("dve", "tensor_copy", (cid,) + ZA, ("cbb",), out=cbb[:, :, 0:sn], in_=cvo[:, :, cs:cs + sn])
                    I("act", "activation", (cid,) + ZA, ("sqb",), out=sqb[:, :, 0:sn], in_=cvo[:, :, cs:cs + sn], func=AF.Square)
                    bm, bq_ = nbank(), nbank()
                    for c in range(4):
                        I("pe", "matmul", ("cbb", "onesdiv") + ZA, (P(bm),), psb[bm][:, 0:sn], lhsT=onesdiv[:], rhs=cbb[:, c, 0:sn],
                          start=(c == 0), stop=(c == 3))
                    for c in range(4):
                        I("pe", "matmul", ("sqb", "onesdiv") + ZA, (P(bq_),), psb[bq_][:, 0:sn], lhsT=onesdiv[:], rhs=sqb[:, c, 0:sn],
                          start=(c == 0), stop=(c == 3))
                    I("act", "activation", (P(bm),) + ZA, ("m32",), out=m32[:, 0:sn], in_=psb[bm][:, 0:sn], func=AF.Copy)
                    I("dve", "tensor_tensor", ("m32",) + ZA, ("var",), out=var[:, 0:sn], in0=m32[:, 0:sn], in1=m32[:, 0:sn], op=ALU.mult)
                    I("dve", "tensor_tensor", (P(bq_), "var") + ZA, ("var",), out=var[:, 0:sn], in0=psb[bq_][:, 0:sn], in1=var[:, 0:sn],
                      op=ALU.subtract)
                    I("act", "activation", ("var",) + ZA, ("var",), out=var[:, 0:sn], in_=var[:, 0:sn], func=AF.Ln, bias=EPS, scale=1.0)
                    I("act", "activation", ("var",) + ZA, ("var",), out=var[:, 0:sn], in_=var[:, 0:sn], func=AF.Exp, scale=-0.5)
                    I("dve", "tensor_tensor", (cid, "m32") + ZA, (cid,), out=cvo[:, :, cs:cs + sn], in0=cvo[:, :, cs:cs + sn],
                      in1=m32[:, 0:sn].unsqueeze(1).to_broadcast([128, 4, sn]), op=ALU.subtract)
                    I("dve", "tensor_tensor", (cid, "var") + ZA, (cid,), out=cvo[:, :, cs:cs + sn], in0=cvo[:, :, cs:cs + sn],
                      in1=var[:, 0:sn].unsqueeze(1).to_broadcast([128, 4, sn]), op=ALU.mult)
                    for c in range(4):
                        I("act", "activation", (cid, "par") + ZA, ("cbb",), out=cbb[:, c, 0:sn], in_=cvo[:, c, cs:cs + sn],
                          func=AF.Silu, bias=P_[:, 40 + c:41 + c], scale=P_[:, 36 + c:37 + c])
                    if bg_pending:
                        m_, e1, e2 = bg_pending.pop(0)
                        in_proj(m_, False, e1, e2)
                    for m in range(4):
                        b = nbank()
                        for kc in range(4):
                            I("pe", "matmul", ("wpw", "cbb") + ZA, (P(b),), psb[b][:, 0:sn], lhsT=wpw[:, m, kc * 128:(kc + 1) * 128],
                              rhs=cbb[:, kc, 0:sn], start=(kc == 0), stop=(kc == 3))
                        I("act", "activation", (P(b), "par") + ZA, ("pw32",), out=pw32[:, 0:sn], in_=psb[b][:, 0:sn],
                          func=AF.Identity, bias=P_[:, 44 + m:45 + m], scale=1.0)
                        if t0 == "s":
                            I("dve", "tensor_tensor", ("pw32", ("gys", m)) + ZA, (("gys", m),), out=gys[:, m, 0:sn], in0=pw32[:, 0:sn],
                              in1=gys[:, m, 0:sn], op=ALU.mult)
                        else:
                            g0 = t0 - 128 + s0
                            I("dve", "tensor_tensor", ("pw32", ("gy", m)) + ZA, (("gy", m),), out=gy[:, m, g0:g0 + sn], in0=pw32[:, 0:sn],
                              in1=gy[:, m, g0:g0 + sn], op=ALU.mult)

        bg_pending = []
        for g in range(8):
            def ev_bg(t0, n, b, g=g):
                I("act", "activation", (P(b),), (("gy", 4 + g),), out=gy[:, 4 + g, t0 - 128:t0 - 128 + n], in_=psb[b][:, 0:n], func=AF.Silu)

            def ev_bg_s(b, g=g):
                I("act", "activation", (P(b),), (("gys", 4 + g),), out=gys[:, 4 + g, :], in_=psb[b][:, 0:16], func=AF.Silu)

            bg_pending.append((22 + g, ev_bg, ev_bg_s))
        conv_stage(0)
        for gi in range(len(groups)):
            if gi + 1 < len(groups):
                conv_stage(gi + 1)
            ln_stage(gi)
        while bg_pending:
            m_, e1, e2 = bg_pending.pop(0)
            in_proj(m_, False, e1, e2)

        ck("A%d" % l)
        phase_barrier()
        cz = Carver()
        qT = cz.get([128, 8, 1408], BF16)
        kT = cz.get([128, TW], BF16)
        vT = cz.get([128, TW], BF16)
        NVT = 4
        vtok = [cz.get([128, 256], BF16) for _ in range(NVT)]
        NPT = 6
        pt = [cz.get([128, 512], BF16) for _ in range(NPT)]
        NDN = 3
        den = [cz.get([128, 512], F32) for _ in range(NDN)]
        cqT = cz.get([128, 4, 1408], BF16)
        for i_ in range(NVT):
            I("pool", "memset", ZA, (("vtok", i_),), vtok[i_][:, 64:192], 1.0)

        def ev_k(t0, n, b):
            I("act", "activation", (P(b),) + ZA, ("kT",), out=kT[:, t0:t0 + n], in_=psb[b][:, 0:n], func=AF.Copy)
            if t0 == 1024:
                I("dve", "tensor_copy", (P(b),), ("k32",), out=k32[:], in_=psb[b][:, 384:512])

        def ev_k_s(b):
            I("act", "activation", (P(b),), ("ksT",), out=ksT[:, 128:144], in_=psb[b][:, 0:16], func=AF.Copy)
            I("dve", "tensor_copy", (P(b),), ("ok32",), out=ok32[:, 112:128], in_=psb[b][:, 0:16])

        def ev_v(t0, n, b):
            I("act", "activation", (P(b),) + ZA, ("vT",), out=vT[:, t0:t0 + n], in_=psb[b][:, 0:n], func=AF.Copy)
            if t0 == 1024:
                I("dve", "tensor_copy", (P(b),), ("v32",), out=v32[:], in_=psb[b][:, 384:512])

        def ev_v_s(b):
            I("act", "activation", (P(b),), ("vsT",), out=vsT[:], in_=psb[b][:, 0:16], func=AF.Copy)
            I("dve", "tensor_copy", (P(b),), ("ov32",), out=ov32[:, 112:128], in_=psb[b][:, 0:16])

        in_proj(12, True, ev_k, ev_k_s)
        in_proj(13, True, ev_v, ev_v_s)
        dma("sp", okp_o[l], k32[:], "k32", r=("k32",))
        dma("sp", ovp_o[l], v32[:], "v32", r=("v32",))
        I("dve", "tensor_copy", ("ck32",), ("ok32",), out=ok32[:, 0:112], in_=ck32[:, 16:128])
        I("dve", "tensor_copy", ("cv32T",), ("ov32",), out=ov32[:, 0:112], in_=cv32T[:, 16:128])
        I("dve", "tensor_copy", ("ck32",), ("ksT",), out=ksT[:, 0:128], in_=ck32[:])
        dma("sp", oks_o[l], ok32[:], "ok32", r=("ok32",))
        dma("sp", ovs_o[l], ov32[:], "ov32", r=("ov32",))
        for g in range(8):
            def ev_q(t0, n, b, g=g):
                I("act", "activation", (P(b),) + ZA, ("qT",), out=qT[:, g, t0 - 128:t0 - 128 + n], in_=psb[b][:, 0:n], func=AF.Copy)

            def ev_q_s(b, g=g):
                I("act", "activation", (P(b),), ("qsT",), out=qsT[:, g, :], in_=psb[b][:, 0:16], func=AF.Copy)

            in_proj(14 + g, False, ev_q, ev_q_s)

        pt_rr = [0]
        dn_rr = [0]
        sb_rr = [0]
        ob_rr = [0]
        I("dve", "tensor_copy", ("esk",), ("sinkrow",), out=sinkrow[0:1, :, :].rearrange("p k (g i) -> p (k g) i", g=8),
          in_=esk[0:1, 0:16].unsqueeze(2).to_broadcast([1, 16, 64]))
        I("dve", "tensor_copy", ("esk",), ("sinkrow_s",), out=sinkrow_s[0:1, :, :].rearrange("p k (g i) -> p (k g) i", g=8),
          in_=esk[0:1, 0:16].unsqueeze(2).to_broadcast([1, 16, 16]))

        def swa_qk(it):
            ni, kv = it["ni"], it["kv"]
            N = 8 * ni
            pts = []
            for bi, (kap, nk, vt, ids, mcol) in enumerate(it["blocks"]):
                b = 3 + (sb_rr[0] % 3)
                sb_rr[0] += 1
                I("pe", "matmul", (ids[0],) + tuple(it["qids"]) + ZA, (P(b),), psb[b][0:nk, 0:N], lhsT=kap, rhs=it["qrhs"],
                  start=True, stop=True)
                pi = pt_rr[0] % NPT
                pt_rr[0] += 1
                p = pt[pi]
                pid = ("pt", pi)
                if mcol is None:
                    I("act", "activation", (P(b),) + ZA, (pid,), out=p[0:nk, 0:N], in_=psb[b][0:nk, 0:N], func=AF.Exp, scale=0.125)
                else:
                    I("act", "activation", (P(b), "maskw") + ZA, (pid,), out=p[0:nk, 0:N], in_=psb[b][0:nk, 0:N], func=AF.Exp,
                      scale=0.125, bias=mcol)
                I("dve", "tensor_tensor", (pid, "E") + ZA, (pid,), out=p[0:nk, 0:N], in0=p[0:nk, 0:N], in1=it["Etab"][0:nk, kv, bi, 0:N],
                  op=ALU.mult)
                pts.append((p, pid, nk, vt, ids))
            it["pts"] = pts

        def swa_pv(it):
            ni, kv = it["ni"], it["kv"]
            N = 8 * ni
            pts = it["pts"]
            bo = ob_rr[0] % 3
            ob_rr[0] += 1
            for bi, (p, pid, nk, vt, ids) in enumerate(pts):
                I("pe", "matmul", (pid,) + tuple(ids) + ZA, (P(bo),), psb[bo][:, 0:N], lhsT=vt[0:nk, kv * 128:(kv + 1) * 128], rhs=p[0:nk, 0:N],
                  start=(bi == 0), stop=False)
            srow = sinkrow if ni == 64 else sinkrow_s
            sid = "sinkrow" if ni == 64 else "sinkrow_s"
            I("pe", "matmul", (sid, "selr"), (P(bo),), psb[bo][:, 0:N], lhsT=selr[0:1, kv, :], rhs=srow[0:1, kv, 0:N], start=False, stop=True)
            lo, hi = kv * 64, kv * 64 + 64
            dlo, dhi = 64 - kv * 64, 128 - kv * 64
            k_ = dn_rr[0] % NDN
            dn_rr[0] += 1
            dn = den[k_]
            did = ("den", k_)
            I("act", "activation", (P(bo),) + ZA, (did,), out=dn[dlo:dhi, 0:N], in_=psb[bo][dlo:dhi, 0:N], func=AF.Ln)
            I("act", "activation", (did,) + ZA, (did,), out=dn[lo:hi, 0:N], in_=dn[dlo:dhi, 0:N], func=AF.Exp, scale=-1.0)
            I("dve", "tensor_tensor", (P(bo), did) + ZA, (did,), out=dn[lo:hi, 0:N], in0=psb[bo][lo:hi, 0:N], in1=dn[lo:hi, 0:N],
              op=ALU.mult)
            gd = it["gdst"](lo, hi)
            I("dve", "tensor_tensor", (did,) + tuple(it["gids"]) + ZA, tuple(it["gids"]), out=gd,
              in0=dn[lo:hi, 0:N].rearrange("p (g i) -> p g i", g=8), in1=gd, op=ALU.mult)

        gy_b_ids = tuple(("gy", 4 + g) for g in range(8))
        vt_counter = [0]

        def mk_vtok_T(w0, nt_, bank):
            i = vt_counter[0] % NVT
            vt_counter[0] += 1
            pT = psb[bank][:, 0:64].bitcast(BF16)
            I("pe", "transpose", ("vT", "identb") + ZA, (P(bank),), pT[0:nt_, :], vT[:, w0:w0 + nt_], identb[:])
            return (i, pT, nt_, bank)

        def mk_vtok_C(h_):
            i, pT, nt_, bank = h_
            I("dve", "tensor_copy", (P(bank),) + ZA, (("vtok", i),), out=vtok[i][0:nt_, 0:64], in_=pT[0:nt_, 0:64])
            I("dve", "tensor_copy", (P(bank),) + ZA, (("vtok", i),), out=vtok[i][0:nt_, 192:256], in_=pT[0:nt_, 64:128])
            return vtok[i], ("vtok", i)

        items = []
        for c in range(m0 // 64, 24):
            wa = c - 2
            q0 = 64 * c - 128
            for kv in range(2):
                items.append(dict(
                    ni=64, kv=kv, c=c, first=(kv == 0), qids=("qT",), Etab=Ep, gids=gy_b_ids,
                    qrhs=qT[kv * 64:kv * 64 + 64, :, q0:q0 + 64],
                    gdst=(lambda lo, hi, q0=q0: gy[lo:hi, 4:12, q0:q0 + 64]),
                    kaps=[(kT[kv * 64:kv * 64 + 64, 64 * wa:64 * wa + 128], 128, maskw[:, wa:wa + 1]),
                          (kT[kv * 64:kv * 64 + 64, 64 * c:64 * c + 64], 64, maskw[0:64, c:c + 1])]))
        bts = nbank()
        pTs = psb[bts][:, 0:64].bitcast(BF16)
        I("pe", "transpose", ("vsT", "identb"), (P(bts),), pTs[0:16, :], vsT[:, :], identb[:])
        I("dve", "tensor_copy", (P(bts),), ("vstok",), out=vstok[:, 0:64], in_=pTs[0:16, 0:64])
        I("dve", "tensor_copy", (P(bts),), ("vstok",), out=vstok[:, 192:256], in_=pTs[0:16, 64:128])
        for kv in range(2):
            items.append(dict(
                ni=16, kv=kv, c=None, first=False, qids=("qsT",), Etab=Es, gids=tuple(("gys", 4 + g) for g in range(8)),
                qrhs=qsT[kv * 64:kv * 64 + 64, :, :], gdst=(lambda lo, hi: gys[lo:hi, 4:12, :]),
                blocks=[(ksT[kv * 64:kv * 64 + 64, 0:128], 128, cvb, ("ksT", "cvb"), None),
                        (ksT[kv * 64:kv * 64 + 64, 128:144], 16, vstok, ("ksT", "vstok"), None)]))
        cg_pending = []
        for h in range(4):
            def ev_cg(t0, n, b, h=h):
                I("act", "activation", (P(b),), (("gy", 12 + h),), out=gy[:, 12 + h, t0 - 128:t0 - 128 + n], in_=psb[b][:, 0:n], func=AF.Silu)

            def ev_cg_s(b, h=h):
                I("act", "activation", (P(b),), (("gys", 12 + h),), out=gys[:, 12 + h, :], in_=psb[b][:, 0:16], func=AF.Silu)

            cg_pending.append((34 + h, ev_cg, ev_cg_s))
        for h in range(4):
            def ev_cq(t0, n, b, h=h):
                I("act", "activation", (P(b),) + ZA, ("cqT",), out=cqT[:, h, t0 - 128:t0 - 128 + n], in_=psb[b][:, 0:n], func=AF.Copy)

            def ev_cq_s(b, h=h):
                I("act", "activation", (P(b),), ("cqs",), out=cqs[:, h, :], in_=psb[b][:, 0:16], func=AF.Copy)

            cg_pending.append((30 + h, ev_cq, ev_cq_s))

        LA = 2
        cur_vt = {}
        for step in range(len(items) + LA):
            if step < len(items):
                it = items[step]
                if it["c"] is not None:
                    if it["first"]:
                        c = it["c"]
                        ha = mk_vtok_T(64 * (c - 2), 128, 6)
                        hb = mk_vtok_T(64 * c, 64, 7)
                        cur_vt["a"] = mk_vtok_C(ha)
                        cur_vt["b"] = mk_vtok_C(hb)
                    (va, vaid), (vb, vbid) = cur_vt["a"], cur_vt["b"]
                    (ka, nka, ma), (kb, nkb, mb_) = it["kaps"]
                    it["blocks"] = [(ka, nka, va, ("kT", vaid), ma), (kb, nkb, vb, ("kT", vbid), mb_)]
                swa_qk(it)
            if step - LA >= 0:
                swa_pv(items[step - LA])
            if cg_pending and step % 5 == 3:
                m_, e1, e2 = cg_pending.pop(0)
                in_proj(m_, False, e1, e2)
        while cg_pending:
            m_, e1, e2 = cg_pending.pop(0)
            in_proj(m_, False, e1, e2)

        ck("B%d" % l)
        phase_barrier()
        cz = Carver()
        NPM = 6
        pm = [cz.get([128, 512], BF16) for _ in range(NPM)]
        dm = [cz.get([128, 512], F32) for _ in range(2)]
        om = [cz.get([128, 512], F32) for _ in range(2)]
        MSC = 128 ** -0.5
        pm_rr = [0]

        def mem_qk(it):
            n, h = it["n"], it["h"]
            ps_ = []
            for mt in range(2):
                b = nbank()
                I("pe", "matmul", tuple(it["mkids"]) + tuple(it["qids"]) + ZA, (P(b),), psb[b][:, 0:n],
                  lhsT=it["mk"][:, h, mt * 128:(mt + 1) * 128], rhs=it["q"], start=True, stop=True)
                i = pm_rr[0] % NPM
                pm_rr[0] += 1
                I("act", "activation", (P(b),) + ZA, (("pm", i),), out=pm[i][:, 0:n], in_=psb[b][:, 0:n], func=AF.Exp, scale=MSC)
                ps_.append(i)
            it["ps"] = ps_

        def mem_pv(it):
            n, h = it["n"], it["h"]
            ps_ = it["ps"]
            bo, bd = nbank(), nbank()
            for mt, i in enumerate(ps_):
                I("pe", "matmul", (("pm", i),) + tuple(it["mvids"]) + ZA, (P(bo),), psb[bo][:, 0:n], lhsT=it["mv"][:, mt, h * 128:(h + 1) * 128],
                  rhs=pm[i][:, 0:n], start=(mt == 0), stop=(mt == 1))
            for mt, i in enumerate(ps_):
                I("pe", "matmul", (("pm", i), "ones") + ZA, (P(bd),), psb[bd][:, 0:n], lhsT=ones[:], rhs=pm[i][:, 0:n],
                  start=(mt == 0), stop=(mt == 1))
            k_ = dm_rr[0] % 2
            dm_rr[0] += 1
            I("act", "activation", (P(bd),) + ZA, (("dm", k_),), out=dm[k_][:, 0:n], in_=psb[bd][:, 0:n], func=AF.Ln)
            I("act", "activation", (("dm", k_),) + ZA, (("dm", k_),), out=dm[k_][:, 0:n], in_=dm[k_][:, 0:n], func=AF.Exp, scale=-1.0)
            I("dve", "tensor_tensor", (P(bo), ("dm", k_)) + ZA, (("om", k_),), out=om[k_][:, 0:n], in0=psb[bo][:, 0:n], in1=dm[k_][:, 0:n],
              op=ALU.mult)
            gd = it["gd"]
            I("dve", "tensor_tensor", (("om", k_), it["gid"]) + ZA, (it["gid"],), out=gd, in0=om[k_][:, 0:n], in1=gd, op=ALU.mult)

        dm_rr = [0]
        mitems = []
        for (t0, n) in tiles_main:
            g0 = t0 - 128
            for h in range(4):
                mitems.append(dict(n=n, h=h, q=cqT[:, h, g0:g0 + n], qids=("cqT",), mk=mkT, mkids=("mkT",),
                                   mv=kvb[:, :, 512:1024], mvids=("kvb",), gd=gy[:, 12 + h, g0:g0 + n], gid=("gy", 12 + h)))
        for h in range(4):
            mitems.append(dict(n=16, h=h, q=cqs[:, h, :], qids=("cqs",), mk=cmkb, mkids=("cmkb",), mv=cmvb, mvids=("cmvb",),
                               gd=gys[:, 12 + h, :], gid=("gys", 12 + h)))
        MLA = 1
        for step in range(len(mitems) + MLA):
            if step < len(mitems):
                mem_qk(mitems[step])
            if step - MLA >= 0:
                mem_pv(mitems[step - MLA])

        ck("C%d" % l)
        phase_barrier()
        cz = Carver()
        rstdn = cz.get([128, 1408], F32)
        memnT_n = cz.get([128, KC, 256], BF16)
        xm_o = cz.get([128, KC, 128], F32)
        czn_o = dict(sq=cz.get([128, KC, 128], BF16), rs=cz.get([128, 128], F32))
        acc = cz.get([128, 1408], F32)
        sqt = [cz.get([128, 512], F32) for _ in range(2)]
        sq_rr = [0]
        xo = [cz.get([128, 512], F32) for _ in range(6)]
        fuse_next = (l + 1 < L)
        if fuse_next:
            Pn = par[(l + 1) % 2]
            dma("sp", Pn[:], params_d[l + 1], ("par", (l + 1) % 2), w=("par",))
        NXO = len(xo)
        xo_rr = [0]
        if fuse_next:
            dma("act", xm_o, memT_d[:, :, 0:128], "xm_o", r=ZA, w=("xm_o",))

        def o_loads(m):
            res = []
            for (t0, n) in tiles_main:
                i = xo_rr[0] % NXO
                xo_rr[0] += 1
                ti = t0 // 512
                dma("sp", xo[i][:, 0:n], xsrc[:, m, t0:t0 + n], ("xo", i), r=(("xd", m, ti),) + ZA, w=(("xo", i),))
                res.append(i)
            return res

        pend = o_loads(0)
        for m in range(KC):
            s = load_w(w_out_d[l, m])
            banks = [nbank() for _ in tiles_main]
            sbk = nbank()
            for kc in range(KC):
                lhs = ring[s][:, kc * 128:(kc + 1) * 128]
                for (t0, n), b in zip(tiles_main, banks):
                    I("pe", "matmul", (("ring", s), ("gy", kc)), (P(b),), psb[b][:, 0:n], lhsT=lhs, rhs=gy[:, kc, t0 - 128:t0 - 128 + n],
                      start=(kc == 0), stop=(kc == KC - 1))
                I("pe", "matmul", (("ring", s), ("gys", kc)), (P(sbk),), psb[sbk][:, 0:16], lhsT=lhs, rhs=gys[:, kc, :],
                  start=(kc == 0), stop=(kc == KC - 1))
            mine = pend
            if m + 1 < KC:
                pend = o_loads(m + 1)
            for (t0, n), b, i in zip(tiles_main, banks, mine):
                ti = t0 // 512
                I("dve", "tensor_tensor", (P(b), ("xo", i)) + ZA, (("xo", i),), out=xo[i][:, 0:n], in0=psb[b][:, 0:n], in1=xo[i][:, 0:n], op=ALU.add)
                dma("sp", xscr[:, m, t0:t0 + n], xo[i][:, 0:n], ("xo", i), r=(("xo", i),) + ZA, w=(("xd", m, ti),))
                if fuse_next:
                    hw = tuple((("h", j), "lo") for j in range(t0 // 128, (t0 + n + 127) // 128))
                    I("act", "activation", (("xo", i), "par") + ZA, hw, out=hT[:, m, t0:t0 + n], in_=xo[i][:, 0:n], func=AF.Copy,
                      scale=Pn[:, m:m + 1])
                    a0 = t0 - 128
                    if m == 0:
                        I("act", "activation", (("xo", i),) + ZA, (("acc", ti),), out=acc[:, a0:a0 + n], in_=xo[i][:, 0:n], func=AF.Square)
                    else:
                        k2 = sq_rr[0] % 2
                        sq_rr[0] += 1
                        I("act", "activation", (("xo", i),) + ZA, (("sqt", k2),), out=sqt[k2][:, 0:n], in_=xo[i][:, 0:n], func=AF.Square)
                        I("dve", "tensor_tensor", (("sqt", k2), ("acc", ti)) + ZA, (("acc", ti),), out=acc[:, a0:a0 + n], in0=acc[:, a0:a0 + n],
                          in1=sqt[k2][:, 0:n], op=ALU.add)
            I("dve", "tensor_tensor", (P(sbk), "xs_res"), ("xs_res",), out=xs_res[:, m, :], in0=psb[sbk][:, 0:16], in1=xs_res[:, m, :], op=ALU.add)
            if fuse_next and m in (1, 7):
                j = 0 if m == 1 else 1
                rms_norm_tile(xm_o, 128, Pn[:, 16:32], "par", memnT_n[:, :, j * 128:(j + 1) * 128], ("xm_o",), ("memn",), czn_o, part="sq")
            if fuse_next and m in (3, 9):
                j = 0 if m == 3 else 1
                rms_norm_tile(xm_o, 128, Pn[:, 16:32], "par", memnT_n[:, :, j * 128:(j + 1) * 128], ("xm_o",), ("memn",), czn_o, part="rest")
                if j == 0:
                    dma("act", xm_o, memT_d[:, :, 128:256], "xm_o", r=ZA, w=("xm_o",))
        if fuse_next:
            for (t0, n) in tiles_main:
                ti = t0 // 512
                a0 = t0 - 128
                b = nbank()
                I("pe", "matmul", (("acc", ti), "onesD32") + ZA, (P(b),), psb[b][:, 0:n], lhsT=onesD32[:], rhs=acc[:, a0:a0 + n], start=True, stop=True)
                I("act", "activation", (P(b),) + ZA, ("rstdn",), out=rstdn[:, a0:a0 + n], in_=psb[b][:, 0:n], func=AF.Ln, bias=EPS, scale=1.0)
                I("act", "activation", ("rstdn",) + ZA, ("rstdn",), out=rstdn[:, a0:a0 + n], in_=rstdn[:, a0:a0 + n], func=AF.Exp, scale=-0.5)
        ck("O%d" % l)
    phase_barrier()
    cz = Carver()
    xn = [cz.get([128, KC, 128], F32) for _ in range(3)]
    czn = dict(sq=cz.get([128, KC, 128], BF16), rs=cz.get([128, 128], F32))
    yos = [cz.get([128, KC, 128], F32) for _ in range(2)]
    ys = cz.get([128, KC, 16], F32)
    rms_norm_tile(xs_res[:], 16, fpar[:, 0:16], "fpar", ys, ("xs_res",), ("ys",), czn)
    dma("sp", ysT_o, ys, "ys", r=both(("ys",)) + ZA)
    for j in range(4, 12):
        bi = j % 3
        yi = j % 2
        yo = yos[yi]
        dma("sp", xn[bi], xscr[:, :, j * 128:(j + 1) * 128], ("xn", bi), r=ZA + tuple(("xd", c, j // 4) for c in range(KC)), w=(("xn", bi),))
        rms_norm_tile(xn[bi], 128, fpar[:, 0:16], "fpar", yo, (("xn", bi),), (("yo", yi),), czn)
        dma("sp", yT_o[:, :, (j - 4) * 128:(j - 3) * 128], yo, ("yo", yi), r=both((("yo", yi),)) + ZA)


def emit(nc, T, es):
    dcnt = T.finalize()
    engs = ["pe", "act", "dve", "pool", "sp"]
    esem = {e: es.enter_context(nc.semaphore("sem_" + e)) for e in engs}
    dsem = {}
    for i, k in enumerate(dcnt.keys()):
        dsem[k] = es.enter_context(nc.semaphore("dsem%d" % i))
    ops = T.ops
    streams = {e: [i for i, o in enumerate(ops) if o["eng"] == e] for e in engs}

    def run_stream(ename, eng):
        seen = {}
        for i in streams[ename]:
            o = ops[i]
            need = {}
            for d in o["deps"]:
                tok = ops[d]["tok"]
                assert tok is not None
                key, val = tok
                if need.get(key, 0) < val:
                    need[key] = val
            for key, val in need.items():
                if seen.get(key, 0) >= val:
                    continue
                seen[key] = val
                sem = esem[key[1]] if key[0] == "e" else dsem[key[1]]
                eng.wait_ge(sem, val)
            ins = o["fn"](eng)
            if o["dkey"] is not None:
                ins.then_inc(dsem[o["dkey"]], 16)
            elif o["inc"]:
                ins.then_inc(esem[ename], 1)
        if ename == "sp":
            for k, v in dcnt.items():
                eng.wait_ge(dsem[k], v)

    with nc.Block() as block:
        @block.tensor
        def _(e):
            run_stream("pe", e)

        @block.scalar
        def _(e):
            run_stream("act", e)

        @block.vector
        def _(e):
            run_stream("dve", e)

        @block.gpsimd
        def _(e):
            run_stream("pool", e)

        @block.sync
        def _(e):
            run_stream("sp", e)
    es.close()
```
